# Optimizing a Trainium2 kernel written in Bass

```python
import math
import jax
import jax.numpy as jnp
from jax import lax
import numpy as np

D_MODEL = 1024
BATCH = 4
SEQ = 8192
DEPTH = 2

CHUNK = 64
N_META = 16
N_PAD = CHUNK - N_META
CONV_K = 4
NORM_EPS = 1e-6

H_A = 6
DH_A = 128
D_A = H_A * DH_A
H_B = 8
DH_B = 64
D_B = H_B * DH_B
R_W = 64
R_A = 64
GN_EPS_B = 64e-5
H_C = 12
DH_C = 64
D_C = H_C * DH_C
G_C = 4
N_C = 128

D_MIX = D_A + D_B + D_C
P_A = 4 * D_A + 2 * H_A
P_B = 4 * D_B + R_W + R_A
P_C = 2 * D_C + 2 * G_C * N_C + H_C
D_PROJ = P_A + P_B + P_C

kernel_name = "hybrid_deltanet_rwkv7_mamba2_trunk"


def _splits(sizes):
    return [int(s) for s in np.cumsum(sizes)[:-1]]


def rms_norm(x, w, eps=NORM_EPS):
    xf = x.astype(jnp.float32)
    y = xf * lax.rsqrt(jnp.mean(xf * xf, axis=-1, keepdims=True) + eps)
    return (y * w.astype(jnp.float32)).astype(x.dtype)


def l2_normalize(x, eps=1e-6):
    xf = x.astype(jnp.float32)
    return xf * lax.rsqrt(jnp.sum(xf * xf, axis=-1, keepdims=True) + eps)


def causal_dwconv(u, w):
    c = u.shape[-1]
    return lax.conv_general_dilated(
        u, w[:, None, :].astype(u.dtype), window_strides=(1,), padding=[(CONV_K - 1, 0)],
        dimension_numbers=("NWC", "WIO", "NWC"), feature_group_count=c)


def pad_front(u, n):
    return jnp.pad(u, [(0, 0), (n, 0)] + [(0, 0)] * (u.ndim - 2))


def token_shift(u):
    return jnp.pad(u, ((0, 0), (1, 0), (0, 0)))[:, :-1]


def gated_delta_rule(q, k, v, g, beta):
    bsz, t_len, n_h, dk = q.shape
    dv = v.shape[-1]
    n_c = t_len // CHUNK

    def to_chunks(u):
        u = u.astype(jnp.float32).reshape((bsz, n_c, CHUNK) + u.shape[2:])
        return jnp.moveaxis(u, 3, 1)

    q, k, v, g, beta = map(to_chunks, (q, k, v, g, beta))
    q = q * (dk ** -0.5)
    gc = jnp.cumsum(g, axis=-1)
    incl = jnp.tril(jnp.ones((CHUNK, CHUNK), dtype=bool))
    strict = jnp.tril(jnp.ones((CHUNK, CHUNK), dtype=bool), -1)
    seg = gc[..., :, None] - gc[..., None, :]
    dmask = jnp.where(incl, jnp.exp(jnp.where(incl, seg, 0.0)), 0.0)
    kb = k * beta[..., None]
    a_mat = jnp.where(strict, jnp.einsum("bhnid,bhnjd->bhnij", kb, k) * dmask, 0.0)
    rhs = jnp.concatenate([v * beta[..., None], kb * jnp.exp(gc)[..., None]], axis=-1)
    sol = lax.linalg.triangular_solve(a_mat + jnp.eye(CHUNK, dtype=jnp.float32), rhs,
                                      left_side=True, lower=True, unit_diagonal=True)
    u_c, w_c = sol[..., :dv], sol[..., dv:]
    attn = jnp.einsum("bhnid,bhnjd->bhnij", q, k) * dmask
    q_dec = q * jnp.exp(gc)[..., None]
    k_dec = k * jnp.exp(gc[..., -1:] - gc)[..., None]
    g_tot = jnp.exp(gc[..., -1])

    def step(state, inp):
        u_i, w_i, q_i, k_i, a_i, g_i = inp
        v_new = u_i - jnp.einsum("bhck,bhkv->bhcv", w_i, state)
        o_i = jnp.einsum("bhck,bhkv->bhcv", q_i, state) + jnp.einsum("bhij,bhjv->bhiv", a_i, v_new)
        state = state * g_i[..., None, None] + jnp.einsum("bhck,bhcv->bhkv", k_i, v_new)
        return state, o_i

    xs = tuple(jnp.moveaxis(u, 2, 0) for u in (u_c, w_c, q_dec, k_dec, attn, g_tot))
    s0 = jnp.zeros((bsz, n_h, dk, dv), jnp.float32)
    _, o = lax.scan(step, s0, xs)
    return jnp.transpose(o, (1, 0, 3, 2, 4)).reshape(bsz, t_len, n_h, dv)


def rwkv7_recurrence(r, w, k, v, a, b):
    bsz, _, n_h, dh = r.shape

    def step(state, inp):
        r_t, w_t, k_t, v_t, a_t, b_t = inp
        sa = jnp.einsum("bhvk,bhk->bhv", state, a_t)
        state = (state * w_t[:, :, None, :] + sa[..., None] * b_t[:, :, None, :]
                 + v_t[..., None] * k_t[:, :, None, :])
        return state, jnp.einsum("bhvk,bhk->bhv", state, r_t)

    xs = tuple(jnp.moveaxis(u.astype(jnp.float32), 1, 0) for u in (r, w, k, v, a, b))
    s0 = jnp.zeros((bsz, n_h, dh, dh), jnp.float32)
    _, y = lax.scan(step, s0, xs)
    return jnp.moveaxis(y, 0, 1)


def _segsum(a):
    t = a.shape[-1]
    cs = jnp.cumsum(a, axis=-1)
    mask = jnp.tril(jnp.ones((t, t), dtype=bool))
    return jnp.where(mask, cs[..., :, None] - cs[..., None, :], -jnp.inf)


def ssd_chunked(x, a, b_mat, c_mat):
    bsz, t_len, n_h, hp = x.shape
    n_c = t_len // CHUNK
    x, b_mat, c_mat = [u.reshape((bsz, n_c, CHUNK) + u.shape[2:]) for u in (x, b_mat, c_mat)]
    a = jnp.moveaxis(a.reshape(bsz, n_c, CHUNK, n_h), 3, 1)
    a_cs = jnp.cumsum(a, axis=-1)
    l_mat = jnp.exp(_segsum(a))
    y_diag = jnp.einsum("bclhn,bcshn,bhcls,bcshp->bclhp", c_mat, b_mat, l_mat, x)
    decay_states = jnp.exp(a_cs[..., -1:] - a_cs)
    states = jnp.einsum("bclhn,bhcl,bclhp->bchpn", b_mat, decay_states, x)
    states = jnp.concatenate([jnp.zeros_like(states[:, :1]), states], axis=1)
    decay_chunk = jnp.exp(_segsum(jnp.pad(a_cs[..., -1], ((0, 0), (0, 0), (1, 0)))))
    states = jnp.einsum("bhzc,bchpn->bzhpn", decay_chunk, states)[:, :-1]
    y_off = jnp.einsum("bclhn,bchpn,bhcl->bclhp", c_mat, states, jnp.exp(a_cs))
    return (y_diag + y_off).reshape(bsz, t_len, n_h, hp)


def gated_deltanet_group(p, conv_w, a_log, dt_bias, norm_w):
    out_dtype = p.dtype
    p = p.astype(jnp.float32)
    bsz, seq_len, _ = p.shape
    qkv, z, a_dt, b_raw = jnp.split(p, _splits([3 * D_A, D_A, H_A, H_A]), axis=-1)
    qkv = jax.nn.silu(causal_dwconv(qkv, conv_w.astype(jnp.float32)))
    q, k, v = [u.reshape(bsz, seq_len, H_A, DH_A) for u in jnp.split(qkv, 3, axis=-1)]
    g = -jnp.exp(a_log.astype(jnp.float32)) * jax.nn.softplus(a_dt + dt_bias.astype(jnp.float32))
    beta = jax.nn.sigmoid(b_raw)
    o = gated_delta_rule(*(pad_front(u, N_PAD) for u in (l2_normalize(q), l2_normalize(k), v, g, beta)))
    o = o[:, N_PAD:]
    o = rms_norm(o, norm_w) * jax.nn.silu(z.reshape(bsz, seq_len, H_A, DH_A))
    return o.reshape(bsz, seq_len, D_A).astype(out_dtype)


def rwkv7_group(p, mu, w0, w2, a0, a2, k_k, k_a, r_k, ln_w, ln_b):
    out_dtype = p.dtype
    p = p.astype(jnp.float32)
    bsz, seq_len, _ = p.shape
    p = p + (token_shift(p) - p) * mu.astype(jnp.float32)
    r, k, v, gate, w_lo, a_lo = jnp.split(p, _splits([D_B] * 4 + [R_W, R_A]), axis=-1)
    log_w = -jnp.exp(-jax.nn.softplus(-(w0 + jnp.tanh(w_lo) @ w2.astype(jnp.float32))) - 0.5)
    a = jax.nn.sigmoid(a0 + a_lo @ a2.astype(jnp.float32))
    heads = lambda u: u.reshape(bsz, seq_len, H_B, DH_B)
    kk = l2_normalize(heads(k * k_k))
    k = k * (1.0 + (a - 1.0) * k_a)
    r, k, v, a, log_w = map(heads, (r, k, v, a, log_w))
    y = rwkv7_recurrence(r, jnp.exp(log_w), k, v, -kk, kk * a)
    mean = jnp.mean(y, axis=-1, keepdims=True)
    var = jnp.mean(jnp.square(y - mean), axis=-1, keepdims=True)
    y = ((y - mean) * lax.rsqrt(var + GN_EPS_B)).reshape(bsz, seq_len, D_B) * ln_w + ln_b
    y = y + (jnp.sum(r * k * r_k, axis=-1, keepdims=True) * v).reshape(bsz, seq_len, D_B)
    return (y * jax.nn.silu(gate)).astype(out_dtype)


def mamba2_group(p, conv_w, conv_b, dt_bias, a_log, d_skip, norm_w):
    out_dtype = p.dtype
    p = p.astype(jnp.float32)
    bsz, seq_len, _ = p.shape
    z, xbc, dt = jnp.split(p, _splits([D_C, D_C + 2 * G_C * N_C, H_C]), axis=-1)
    xbc = jax.nn.silu(causal_dwconv(xbc, conv_w.astype(jnp.float32)) + conv_b)
    xs, b_mat, c_mat = jnp.split(xbc, _splits([D_C, G_C * N_C, G_C * N_C]), axis=-1)
    xs = xs.reshape(bsz, seq_len, H_C, DH_C)
    to_heads = lambda u: jnp.repeat(u.reshape(bsz, seq_len, G_C, N_C), H_C // G_C, axis=2)
    dt = jax.nn.softplus(dt + dt_bias.astype(jnp.float32))
    a = -jnp.exp(a_log.astype(jnp.float32))
    y = ssd_chunked(*(pad_front(u, N_PAD) for u in (xs * dt[..., None], a * dt, to_heads(b_mat), to_heads(c_mat))))
    y = (y[:, N_PAD:] + d_skip[:, None] * xs).reshape(bsz, seq_len, D_C)
    y = y * jax.nn.silu(z)
    yg = y.reshape(bsz, seq_len, G_C, D_C // G_C)
    yg = yg * lax.rsqrt(jnp.mean(yg * yg, axis=-1, keepdims=True) + NORM_EPS)
    return (yg.reshape(bsz, seq_len, D_C) * norm_w).astype(out_dtype)


def setup_inputs(seed: int = 0) -> dict:
    key = jax.random.key(seed)
    ks = jax.random.split(key, 26)
    f32 = jnp.float32

    def nrm(k, shape, scale):
        return scale * jax.random.normal(k, shape, f32)

    def gain(k, shape):
        return 1.0 + 0.05 * jax.random.normal(k, shape, f32)

    def a_log_init(k, shape):
        return jnp.log(jax.random.uniform(k, shape, f32, minval=1.0, maxval=16.0))

    def dt_bias_init(k, shape):
        dt = jnp.exp(jax.random.uniform(k, shape, f32, minval=math.log(1e-3), maxval=math.log(1e-1)))
        return dt + jnp.log(-jnp.expm1(-dt))

    conv_c = D_C + 2 * G_C * N_C
    return {
        "x": nrm(ks[0], (BATCH, SEQ, D_MODEL), 1.0),
        "meta_tokens": nrm(ks[1], (N_META, D_MODEL), 1.0),
        "norm_pre": gain(ks[2], (DEPTH, D_MODEL)),
        "norm_post": gain(ks[3], (DEPTH, D_MODEL)),
        "w_in": nrm(ks[4], (DEPTH, D_MODEL, D_PROJ), D_MODEL ** -0.5),
        "w_out": nrm(ks[5], (DEPTH, D_MIX, D_MODEL), D_MIX ** -0.5),
        "dn_conv": nrm(ks[6], (DEPTH, CONV_K, 3 * D_A), CONV_K ** -0.5),
        "dn_A_log": a_log_init(ks[7], (DEPTH, H_A)),
        "dn_dt_bias": dt_bias_init(ks[8], (DEPTH, H_A)),
        "dn_norm": gain(ks[9], (DEPTH, DH_A)),
        "rw_mu": jax.random.uniform(ks[10], (DEPTH, P_B), f32),
        "rw_w0": jax.random.uniform(ks[11], (DEPTH, D_B), f32, minval=-2.0, maxval=2.0),
        "rw_w2": nrm(ks[12], (DEPTH, R_W, D_B), 0.5 * R_W ** -0.5),
        "rw_a0": nrm(ks[13], (DEPTH, D_B), 0.1),
        "rw_a2": nrm(ks[14], (DEPTH, R_A, D_B), 0.5 * R_A ** -0.5),
        "rw_k_k": 0.85 + 0.05 * jax.random.normal(ks[15], (DEPTH, D_B), f32),
        "rw_k_a": gain(ks[16], (DEPTH, D_B)),
        "rw_r_k": nrm(ks[17], (DEPTH, H_B, DH_B), 0.1),
        "rw_ln_w": gain(ks[18], (DEPTH, D_B)),
        "rw_ln_b": nrm(ks[19], (DEPTH, D_B), 0.01),
        "mb_conv": nrm(ks[20], (DEPTH, CONV_K, conv_c), CONV_K ** -0.5),
        "mb_conv_b": nrm(ks[21], (DEPTH, conv_c), 0.01),
        "mb_dt_bias": dt_bias_init(ks[22], (DEPTH, H_C)),
        "mb_A_log": a_log_init(ks[23], (DEPTH, H_C)),
        "mb_D": gain(ks[24], (DEPTH, H_C)),
        "mb_norm": gain(ks[25], (DEPTH, D_C)),
    }


def reference(x, meta_tokens, norm_pre, norm_post, w_in, w_out,
              dn_conv, dn_A_log, dn_dt_bias, dn_norm,
              rw_mu, rw_w0, rw_w2, rw_a0, rw_a2, rw_k_k, rw_k_a, rw_r_k, rw_ln_w, rw_ln_b,
              mb_conv, mb_conv_b, mb_dt_bias, mb_A_log, mb_D, mb_norm):
    bsz = x.shape[0]
    meta = jnp.broadcast_to(meta_tokens.astype(x.dtype)[None], (bsz, N_META, D_MODEL))
    h = jnp.concatenate([meta, x], axis=1)
    for l in range(DEPTH):
        hn = rms_norm(h, norm_pre[l])
        proj = jnp.einsum("bld,dp->blp", hn, w_in[l])
        p_a, p_b, p_c = jnp.split(proj, _splits([P_A, P_B, P_C]), axis=-1)
        mixed = jnp.concatenate([
            gated_deltanet_group(p_a, dn_conv[l], dn_A_log[l], dn_dt_bias[l], dn_norm[l]),
            rwkv7_group(p_b, rw_mu[l], rw_w0[l], rw_w2[l], rw_a0[l], rw_a2[l],
                        rw_k_k[l], rw_k_a[l], rw_r_k[l], rw_ln_w[l], rw_ln_b[l]),
            mamba2_group(p_c, mb_conv[l], mb_conv_b[l], mb_dt_bias[l], mb_A_log[l], mb_D[l], mb_norm[l]),
        ], axis=-1)
        out = jnp.einsum("blm,md->bld", mixed, w_out[l])
        h = h + rms_norm(out, norm_post[l])
    return h[:, N_META:]
```

```python
import numpy as np
from contextlib import ExitStack
import concourse.bass as bass
import concourse.mybir as mybir
from concourse.bass_utils import run_bass_kernel_spmd

F32 = mybir.dt.float32
BF16 = mybir.dt.bfloat16
AF = mybir.ActivationFunctionType
ALU = mybir.AluOpType

D = 1024
DEPTH = 2
C = 64
NPADROWS = 128
CH = 128
H_A, DH_A = 6, 128
D_A = 768
H_B, DH_B = 8, 64
D_B = 512
H_C, DH_C = 12, 64
D_C = 768
G_C, N_C = 4, 128
P_A = 4 * D_A + 2 * H_A
P_B = 4 * D_B + 128
P_C = 2 * D_C + 2 * G_C * N_C + H_C
OFF_B = P_A
OFF_C = P_A + P_B
D_PROJ = P_A + P_B + P_C
EPOCH = 8192
NEG = -1.0e30


class Tok:
    __slots__ = ("w", "r", "excl")

    def __init__(self, excl=False):
        self.w = None
        self.r = {}
        self.excl = excl


class Sched:
    def __init__(self, nc, stack):
        self.nc = nc
        self.stack = stack
        self.E = {"pe": nc.tensor, "dve": nc.vector, "act": nc.scalar, "pool": nc.gpsimd, "sp": nc.sync}
        self.cnt = {e: 0 for e in self.E}
        self.sems = {e: [] for e in self.E}
        self.dma_sems = [stack.enter_context(nc.semaphore("dq%d" % i)) for i in range(8)]
        self.dma_cnt = [0] * 8
        self.ndma = 0
        self.waited = {e: {} for e in self.E}

    def _sem(self, eng, idx):
        ep = idx // EPOCH
        while len(self.sems[eng]) <= ep:
            self.sems[eng].append(self.stack.enter_context(self.nc.semaphore("s_%s_%d" % (eng, len(self.sems[eng])))))
        return self.sems[eng][ep], (idx % EPOCH) + 1

    def _wait(self, eng, dep):
        pe, pidx, dq = dep
        if dq is not None:
            key = ("dq", dq[0])
            if self.waited[eng].get(key, 0) >= dq[1]:
                return
            self.waited[eng][key] = dq[1]
            self.E[eng].wait_ge(self.dma_sems[dq[0]], dq[1])
            return
        if pe == eng and eng == "pe":
            return
        if self.waited[eng].get(pe, -1) >= pidx:
            return
        self.waited[eng][pe] = pidx
        s, v = self._sem(pe, pidx)
        self.E[eng].wait_ge(s, v)

    def op(self, eng, fn, reads=(), writes=(), dma=False):
        deps = []
        for t in reads:
            if t.w is not None:
                deps.append(t.w)
            if t.excl:
                for e, d in t.r.items():
                    if e != eng:
                        deps.append(d)
        for t in writes:
            if t.w is not None:
                deps.append(t.w)
            for e, d in t.r.items():
                if e != eng or dma:
                    deps.append(d)
        for d in deps:
            self._wait(eng, d)
        ins = fn(self.E[eng])
        idx = self.cnt[eng]
        self.cnt[eng] += 1
        if dma:
            q = self.ndma % 8
            self.ndma += 1
            self.dma_cnt[q] += 16
            ins.then_inc(self.dma_sems[q], 16)
            me = (eng, idx, (q, self.dma_cnt[q]))
            rk = "dma%d" % idx
        else:
            s, v = self._sem(eng, idx)
            ins.then_inc(s, 1)
            me = (eng, idx, None)
            rk = eng
        for t in reads:
            t.r[rk] = me
        for t in writes:
            t.w = me
            t.r = {}
        return me


class ColReg:
    def __init__(self):
        self.names = {}
        self.cols = []

    def add(self, name, vec):
        self.names[name] = len(self.cols)
        self.cols.append(np.asarray(vec, np.float32).reshape(128))


def pack_pv(l, P):
    R = ColReg()
    for k in range(8):
        R.add("npre%d" % k, P["norm_pre"][l][k * 128:(k + 1) * 128])
    for h in range(H_A):
        for pi, pn in enumerate("qkv"):
            for j in range(4):
                R.add("gcw_%s%d_%d" % (pn, h, j), P["dn_conv"][l][j, pi * D_A + h * 128: pi * D_A + (h + 1) * 128])
    R.add("gnorm", P["dn_norm"][l])
    for j in range(4):
        for pi, pn in enumerate("rkvg"):
            R.add("mu_%s%d" % (pn, j), P["rw_mu"][l][pi * D_B + j * 128: pi * D_B + (j + 1) * 128])
        R.add("w0_%d" % j, P["rw_w0"][l][j * 128:(j + 1) * 128])
        R.add("a0_%d" % j, P["rw_a0"][l][j * 128:(j + 1) * 128])
        R.add("kk_%d" % j, P["rw_k_k"][l][j * 128:(j + 1) * 128])
        R.add("ka_%d" % j, P["rw_k_a"][l][j * 128:(j + 1) * 128])
        R.add("rk_%d" % j, P["rw_r_k"][l].reshape(-1)[j * 128:(j + 1) * 128])
    R.add("mu_lora", P["rw_mu"][l][4 * D_B:4 * D_B + 128])
    for b in range(14):
        for j in range(4):
            R.add("mcw%d_%d" % (b, j), P["mb_conv"][l][j, b * 128:(b + 1) * 128])
        R.add("mcb%d" % b, P["mb_conv_b"][l][b * 128:(b + 1) * 128])
    for b in range(6):
        R.add("mnorm%d" % b, P["mb_norm"][l][b * 128:(b + 1) * 128])
    return R


ROW_SMB = 0
ROW_ALOG = 24
ROW_D = 42
ROW_LNW = 64
ROW_LNB = 64 + 512
ROW_NPOST = 64 + 1024
NROW = 64 + 2048


def pack_rows(l, P):
    r = np.zeros((NROW,), np.float32)
    r[0:6] = P["dn_dt_bias"][l]
    r[12:24] = P["mb_dt_bias"][l]
    r[24:30] = P["dn_A_log"][l]
    r[30:42] = P["mb_A_log"][l]
    r[42:54] = P["mb_D"][l]
    r[ROW_LNW:ROW_LNW + 512] = P["rw_ln_w"][l]
    r[ROW_LNB:ROW_LNB + 512] = P["rw_ln_b"][l]
    r[ROW_NPOST:ROW_NPOST + 1024] = P["norm_post"][l]
    return np.ascontiguousarray(np.broadcast_to(r[None, :], (128, NROW)))


def make_consts():
    c = {}
    c["ident"] = np.eye(128, dtype=np.float32)
    idx = np.arange(128)
    tri = (idx[:, None] <= idx[None, :]).astype(np.float32)
    sut = (idx[:, None] > idx[None, :]).astype(np.float32)
    negt = np.where(idx[:, None] > idx[None, :], NEG, 0.0).astype(np.float32)
    strict = (idx[:, None] < idx[None, :]).astype(np.float32)
    c["tri"] = tri
    c["sut"] = sut
    c["negt"] = negt
    c["strict"] = strict
    c["mask4"] = np.concatenate([strict, strict, tri, tri], 1)
    blk2 = np.zeros((128, 128), np.float32)
    blk2[:64, :64] = 1.0
    blk2[64:, 64:] = 1.0
    c["blk2"] = blk2
    cm = np.ones((128, 512), np.float32)
    cm[:, ::CH] = 0.0
    c["cmask"] = cm
    names = ["ident", "tri", "sut", "negt", "strict", "mask4", "blk2", "cmask"]
    offs = {}
    o = 0
    for n in names:
        offs[n] = (o, c[n].shape[1])
        o += c[n].shape[1]
    arr = np.concatenate([c[n] for n in names], 1).astype(np.float32)
    return arr, offs


def build(seq, pvnames, coffs, ncst, npv, flags=None, debug_mixed=False):
    flags = flags or {"gdn": True, "rwkv": True, "ssd": True}
    ROWS = NPADROWS + seq
    tiles = [(0, 128)] + [(128 + i * 512, 512) for i in range(seq // 512)]
    nc = bass.Bass("TRN2", target_bir_lowering=False)
    x_d = nc.dram_tensor("x", [seq, D], F32, kind="ExternalInput").ap()
    meta_d = nc.dram_tensor("meta", [16, D], F32, kind="ExternalInput").ap()
    win_d = nc.dram_tensor("win", [DEPTH, D, D_PROJ], F32, kind="ExternalInput").ap()
    wout_d = nc.dram_tensor("wout", [DEPTH, 2 * D, D], F32, kind="ExternalInput").ap()
    pv_d = nc.dram_tensor("pv", [DEPTH, 128, npv], F32, kind="ExternalInput").ap()
    rows_d = nc.dram_tensor("rows", [DEPTH, 128, NROW], F32, kind="ExternalInput").ap()
    lora_d = nc.dram_tensor("lora", [DEPTH, 128, 512], F32, kind="ExternalInput").ap()
    cst_d = nc.dram_tensor("cst", [128, ncst], F32, kind="ExternalInput").ap()
    y_d = nc.dram_tensor("y", [seq, D], F32, kind="ExternalOutput").ap()
    h1_d = nc.dram_tensor("h1", [ROWS, D], F32, kind="Internal").ap()
    if debug_mixed:
        dbg_d = nc.dram_tensor("dbg", [DEPTH, 16, 128, ROWS], F32, kind="ExternalOutput").ap()

    st = ExitStack()
    with st:
        S = Sched(nc, st)
        uid = [0]

        def sb(shape, dt=F32, name=None):
            uid[0] += 1
            return st.enter_context(nc.sbuf_tensor("%s_%d" % (name or "t", uid[0]), list(shape), dt))

        class RP:
            def __init__(self, shape, dt, n, name="p"):
                self.items = [(sb(shape, dt, name), Tok()) for _ in range(n)]
                self.i = 0

            def get(self):
                it = self.items[self.i % len(self.items)]
                self.i += 1
                return it

        psum = st.enter_context(nc.psum_tensor("psum", [128, 4096], F32))
        ptok = [Tok(excl=True) for _ in range(8)]
        pctr = [0, 0, 0]

        def bank(hi=False):
            if hi:
                b = 6 + pctr[1] % 2
                pctr[1] += 1
            else:
                b = pctr[0] % 6
                pctr[0] += 1
            return b * 512, ptok[b]

        def bank2():
            b = 2 * (pctr[2] % 3)
            pctr[2] += 1
            return b * 512, [ptok[b], ptok[b + 1]]

        def op(eng, fn, R=(), W=()):
            return S.op(eng, fn, reads=R, writes=W)

        def dma(out, in_, R=(), W=()):
            return S.op("sp", lambda e: e.dma_start(out=out, in_=in_), reads=R, writes=W, dma=True)

        def mm(out, lhsT, rhs, R, W, start=True, stop=True):
            return S.op("pe", lambda e: e.matmul(out, lhsT=lhsT, rhs=rhs, start=start, stop=stop), reads=R, writes=W)

        cst = sb([128, ncst], F32, "cst")
        t_cst = Tok()
        dma(cst[:], cst_d, W=[t_cst])

        def CST(name):
            o, w = coffs[name]
            return cst[:, o:o + w]

        identf = CST("ident")
        identb = sb([128, 128], BF16, "identb")
        onesb = sb([128, 128], BF16, "onesb")
        blk2b = sb([128, 128], BF16, "blk2b")
        onesf = sb([128, 128], F32, "onesf")
        t_c2 = Tok()
        op("dve", lambda e: e.tensor_copy(out=identb[:], in_=CST("ident")), R=[t_cst], W=[t_c2])
        op("dve", lambda e: e.tensor_copy(out=blk2b[:], in_=CST("blk2")), R=[t_cst], W=[t_c2])
        op("pool", lambda e: e.memset(onesb[:], 1.0), W=[t_c2])
        op("pool", lambda e: e.memset(onesf[:], 1.0), W=[t_c2])
        TC = [t_cst, t_c2]

        pv = sb([128, npv + 64], F32, "pv")
        t_pv = Tok()
        rows = sb([128, NROW], F32, "rows")
        t_rows = Tok()
        nega = sb([128, 18], F32, "nega")
        lwb = sb([128, 512], BF16, "lwb")
        t_lw = Tok()
        wsm = sb([128, 8, 24], BF16, "wsm")
        t_wsm = Tok()
        woutb = sb([128, 16, 1024], BF16, "woutb")
        t_wout = Tok()
        hnT = sb([128, 8, 512], BF16, "hnT")
        t_hnT = Tok()
        mixedT = sb([128, 16, 512], BF16, "mixedT")
        t_mixed = [Tok() for _ in range(16)]

        def PV(name):
            c = pvnames[name]
            return pv[:, c:c + 1]

        def PVX(i):
            return pv[:, npv + i:npv + i + 1]

        xin_p = RP([128, 1024], F32, 2, "xin")
        xn_p = RP([128, 1024], F32, 2, "xn")
        wld_p = RP([128, 8, 128], F32, 2, "wld")
        wb_p = RP([128, 8, 128], BF16, 2, "wb")
        big_p = RP([128, 516], F32, 16, "big")
        tmp_p = RP([128, 516], F32, 4, "tmpb")
        bigb_p = RP([128, 512], BF16, 12, "bigb")
        lo_t = (sb([128, 512], BF16, "lo"), Tok())
        egt_p = RP([128, 18], F32, 6, "egt")
        col_p = RP([128, 8], F32, 8, "col")
        sm_p = RP([128, 96], F32, 6, "sm")
        p128f = RP([128, 128], F32, 12, "p128f")
        p128b = RP([128, 128], BF16, 8, "p128b")
        p256f = RP([128, 256], F32, 10, "p256f")
        p64f = RP([128, 64], F32, 10, "p64f")
        p64b = RP([128, 64], BF16, 8, "p64b")
        p384b = RP([128, 384], BF16, 3, "p384b")
        ytok_p = RP([128, 768], F32, 2, "ytok")
        xt_p = RP([128, 128], F32, 7, "xt")
        xd_p = RP([128, 128], BF16, 7, "xd")

        Sg = [(sb([128, 128], F32, "Sg"), sb([128, 128], BF16, "Sgb"), Tok()) for _ in range(H_A)]
        Sr = [(sb([128, 64], F32, "Sr"), sb([128, 64], BF16, "Srb"), Tok()) for _ in range(4)]
        Sm = [(sb([128, 64], F32, "Sm"), sb([128, 64], BF16, "Smb"), Tok()) for _ in range(H_C)]
        carry_g = [[(sb([128, 4], F32, "cg"), Tok()) for _ in range(3)] for _ in range(H_A)]
        carry_r = [[(sb([128, 4], F32, "cr"), Tok()) for _ in range(4)] for _ in range(4)]
        carry_l = (sb([128, 4], F32, "cl"), Tok())
        carry_m = [(sb([128, 4], F32, "cm"), Tok()) for _ in range(14)]

        h1tok = [[Tok() for _ in range(4)] for _ in tiles]

        def rstd_from_ss(ss_ap, out_ap, n, eps, toks_r, tok_w, pq=slice(0, 128)):
            tmp, ttmp = col_p.get()
            op("act", lambda e: e.activation(out=tmp[pq, 0:1], in_=ss_ap, func=AF.Sqrt, scale=1.0 / n, bias=float(eps)),
               R=toks_r, W=[ttmp])
            op("dve", lambda e: e.reciprocal(out=out_ap, in_=tmp[pq, 0:1]), R=[ttmp], W=[tok_w])

        def proj_block(l, col0, T, evac):
            wl, twl = wld_p.get()
            src = win_d[l].rearrange("(kc p) c -> p kc c", p=128)[:, :, col0:col0 + 128]
            dma(wl[:], src, W=[twl])
            wb, twb = wb_p.get()
            op("pool", lambda e: e.tensor_copy(out=wb[:], in_=wl[:]), R=[twl], W=[twb])
            po, tp = bank()
            for kc in range(8):
                mm(psum[:, po:po + T], wb[:, kc, :], hnT[:, kc, 0:T], R=[twb, t_hnT], W=[tp], start=(kc == 0), stop=(kc == 7))
            evac(psum[:, po:po + T], tp)

        def conv_block(T, u, tu, cw_names, carry, bias_name, out_ap, tok_out):
            cbuf, tcar = carry
            acc, tacc = tmp_p.get()
            op("dve", lambda e: e.tensor_scalar(acc[:, 0:T], u[:, 0:T], PV(cw_names[0]), None, op0=ALU.mult), R=[tu, t_pv], W=[tacc])
            for j in range(1, 4):
                op("dve", lambda e, j=j: e.scalar_tensor_tensor(out=acc[:, 0:T], in0=u[:, j:j + T], scalar=PV(cw_names[j]),
                                                                 in1=acc[:, 0:T], op0=ALU.mult, op1=ALU.add), R=[tu, tacc, t_pv], W=[tacc])
            op("pool", lambda e: e.tensor_copy(out=cbuf[:, 0:3], in_=u[:, T:T + 3]), R=[tu], W=[tcar])
            if bias_name is None:
                op("act", lambda e: e.activation(out=out_ap, in_=acc[:, 0:T], func=AF.Silu), R=[tacc], W=[tok_out])
            else:
                op("act", lambda e: e.activation(out=out_ap, in_=acc[:, 0:T], func=AF.Silu, bias=PV(bias_name)), R=[tacc, t_pv], W=[tok_out])

        def proj_conv(l, col0, T, carry, cw_names, bias_name, out_ap, tok_out):
            u, tu = tmp_p.get()
            cbuf, tcar = carry
            op("pool", lambda e: e.tensor_copy(out=u[:, 0:3], in_=cbuf[:, 0:3]), R=[tcar], W=[tu])
            proj_block(l, col0, T, lambda ps, tp: op("act", lambda e: e.activation(out=u[:, 3:3 + T], in_=ps, func=AF.Copy), R=[tp], W=[tu]))
            conv_block(T, u, tu, cw_names, carry, bias_name, out_ap, tok_out)

        def proj_shift(l, col0, T, carry, mu_name, out_ap, tok_out):
            u, tu = tmp_p.get()
            cbuf, tcar = carry
            op("pool", lambda e: e.tensor_copy(out=u[:, 0:1], in_=cbuf[:, 0:1]), R=[tcar], W=[tu])
            proj_block(l, col0, T, lambda ps, tp: op("act", lambda e: e.activation(out=u[:, 1:1 + T], in_=ps, func=AF.Copy), R=[tp], W=[tu]))
            d, td = tmp_p.get()
            op("pool", lambda e: e.tensor_tensor(out=d[:, 0:T], in0=u[:, 0:T], in1=u[:, 1:1 + T], op=ALU.subtract), R=[tu], W=[td])
            op("pool", lambda e: e.tensor_copy(out=cbuf[:, 0:1], in_=u[:, T:T + 1]), R=[tu], W=[tcar])
            op("dve", lambda e: e.scalar_tensor_tensor(out=out_ap, in0=d[:, 0:T], scalar=PV(mu_name), in1=u[:, 1:1 + T],
                                                       op0=ALU.mult, op1=ALU.add), R=[td, tu, t_pv], W=[tok_out])

        def decay_mask(gcol_ap, tg):
            gm, tgm = p128f.get()
            op("pool", lambda e: e.tensor_scalar(gm[:, :], CST("tri"), gcol_ap, None, op0=ALU.mult), R=[tg] + TC, W=[tgm])
            po, tp = bank()
            mm(psum[:, po:po + 128], CST("sut"), gm[:, :], R=[tgm] + TC, W=[tp], start=True, stop=False)
            mm(psum[:, po:po + 128], identf, CST("negt"), R=TC, W=[tp], start=False, stop=True)
            dt_, tdt = p128f.get()
            op("act", lambda e: e.activation(out=dt_[:, :], in_=psum[:, po:po + 128], func=AF.Exp), R=[tp], W=[tdt])
            return dt_, tdt

        def neumann_multi(items):
            st_ = []
            for (PT0, tpt0, Y0, ty0, width) in items:
                po, tp = bank()
                mm(psum[:, po:po + 128], PT0, identf, R=[tpt0] + TC, W=[tp])
                st_.append([None, None, Y0, ty0, width, po, tp, PT0, tpt0])
            for it in st_:
                PP, tpp = p256f.get()
                po, tp, PT0, tpt0 = it[5], it[6], it[7], it[8]
                op("act", lambda e, PP=PP, po=po: e.activation(out=PP[:, 0:128], in_=psum[:, po:po + 128], func=AF.Copy), R=[tp], W=[tpp])
                op("pool", lambda e, PP=PP, PT0=PT0: e.tensor_copy(out=PP[:, 128:256], in_=PT0), R=[tpt0], W=[tpp])
                it[0], it[1] = PP, tpp
            for k in range(7):
                banks = []
                for it in st_:
                    PP, tpp, Y, ty, width = it[0], it[1], it[2], it[3], it[4]
                    P_ = PP[:, 0:128]
                    PT_ = PP[:, 128:256]
                    po, tp = bank()
                    mm(psum[:, po:po + width], PT_, Y[:, 0:width], R=[tpp, ty], W=[tp])
                    po2 = tp2 = None
                    if k < 6:
                        po2, tp2 = bank()
                        mm(psum[:, po2:po2 + 128], PT_, P_, R=[tpp], W=[tp2])
                        mm(psum[:, po2 + 128:po2 + 256], P_, PT_, R=[tpp], W=[tp2])
                    banks.append((po, tp, po2, tp2))
                for it, (po, tp, po2, tp2) in zip(st_, banks):
                    Y, ty, width = it[2], it[3], it[4]
                    Yn, tyn = p256f.get()
                    op("dve", lambda e, Y=Y, Yn=Yn, po=po, width=width: e.tensor_tensor(out=Yn[:, 0:width], in0=psum[:, po:po + width], in1=Y[:, 0:width], op=ALU.add),
                       R=[tp, ty], W=[tyn])
                    if k < 6:
                        PPn, tppn = p256f.get()
                        op("act", lambda e, PPn=PPn, po2=po2: e.activation(out=PPn[:, :], in_=psum[:, po2:po2 + 256], func=AF.Copy), R=[tp2], W=[tppn])
                        it[0], it[1] = PPn, tppn
                    it[2], it[3] = Yn, tyn
            return [(it[2], it[3]) for it in st_]

        for l in range(DEPTH):
            dma(pv[:, 0:npv], pv_d[l], W=[t_pv])
            dma(rows[:], rows_d[l], W=[t_rows])
            for j in range(4):
                op("dve", lambda e, j=j: e.tensor_scalar(PVX(j), PV("ka_%d" % j), -1.0, 1.0, op0=ALU.mult, op1=ALU.add), R=[t_pv], W=[t_pv])
            op("act", lambda e: e.activation(out=nega[:, :], in_=rows[:, ROW_ALOG:ROW_ALOG + 18], func=AF.Exp), R=[t_rows], W=[t_rows])
            op("dve", lambda e: e.tensor_scalar(nega[:, :], nega[:, :], -1.0, None, op0=ALU.mult), R=[t_rows], W=[t_rows])
            lw32, tlw32 = big_p.get()
            dma(lw32[:, 0:512], lora_d[l], W=[tlw32])
            op("pool", lambda e: e.tensor_copy(out=lwb[:], in_=lw32[:, 0:512]), R=[tlw32], W=[t_lw])
            ws32, tws32 = big_p.get()
            wv = ws32[:, 0:192].rearrange("p (k c) -> p k c", k=8)
            winr = win_d[l].rearrange("(kc p) c -> p kc c", p=128)
            dma(wv[:, :, 0:12], winr[:, :, 3072:3084], W=[tws32])
            dma(wv[:, :, 12:24], winr[:, :, OFF_C + 2560:OFF_C + 2572], W=[tws32])
            op("pool", lambda e: e.tensor_copy(out=wsm[:], in_=wv), R=[tws32], W=[t_wsm])
            for b in range(16):
                wo32, two32 = xin_p.get()
                dma(wo32[:], wout_d[l][b * 128:(b + 1) * 128, :], W=[two32])
                op("pool", lambda e, b=b, wo32=wo32: e.tensor_copy(out=woutb[:, b, :], in_=wo32[:]), R=[two32], W=[t_wout])
            for (s32, s16, ts) in Sg + Sr + Sm:
                op("pool", lambda e, s32=s32: e.memset(s32[:], 0.0), W=[ts])
                op("pool", lambda e, s16=s16: e.memset(s16[:], 0.0), W=[ts])
            for cb, tcb in [c for hh in carry_g for c in hh] + [c for hh in carry_r for c in hh] + [carry_l] + carry_m:
                op("pool", lambda e, cb=cb: e.memset(cb[:], 0.0), W=[tcb])

            for ti, (r0, T) in enumerate(tiles):
                NCH = T // CH
                nsub = T // 128

                def load_h(si, dst, tdst):
                    g0 = r0 + si * 128
                    if l == 0:
                        if ti == 0:
                            op("pool", lambda e: e.memset(dst[:, :], 0.0), W=[tdst])
                            dma(dst[112:128, :], meta_d, W=[tdst])
                        else:
                            dma(dst[:, :], x_d[g0 - 128:g0, :], W=[tdst])
                    else:
                        dma(dst[:, :], h1_d[g0:g0 + 128, :], R=[h1tok[ti][si]], W=[tdst])

                for si in range(nsub):
                    xs_, txs = xin_p.get()
                    load_h(si, xs_, txs)
                    xn_, txn = xn_p.get()
                    cl, tcl = col_p.get()
                    op("act", lambda e: e.activation(out=xn_[:, :], in_=xs_[:, :], func=AF.Square, accum_out=cl[:, 0:1]), R=[txs], W=[txn, tcl])
                    rstd_from_ss(cl[:, 0:1], cl[:, 1:2], D, 1e-6, [tcl], tcl)
                    op("dve", lambda e: e.tensor_scalar(xn_[:, :], xs_[:, :], cl[:, 1:2], None, op0=ALU.mult), R=[txs, tcl], W=[txn])
                    for half in range(2):
                        po, tp = bank()
                        for k4 in range(4):
                            kc = half * 4 + k4
                            mm(psum[:, po + k4 * 128: po + (k4 + 1) * 128], xn_[:, kc * 128:(kc + 1) * 128], identf, R=[txn] + TC, W=[tp])
                        for k4 in range(4):
                            kc = half * 4 + k4
                            if k4 % 2:
                                op("act", lambda e, kc=kc, k4=k4, po=po: e.activation(out=hnT[:, kc, si * 128:(si + 1) * 128], in_=psum[:, po + k4 * 128: po + (k4 + 1) * 128],
                                                                                      func=AF.Copy, scale=PV("npre%d" % kc)), R=[tp, t_pv], W=[t_hnT])
                            else:
                                op("dve", lambda e, kc=kc, k4=k4, po=po: e.tensor_scalar(hnT[:, kc, si * 128:(si + 1) * 128], psum[:, po + k4 * 128: po + (k4 + 1) * 128],
                                                                                         PV("npre%d" % kc), None, op0=ALU.mult), R=[tp, t_pv], W=[t_hnT])

                smc = []
                egt = []
                for c in range(NCH):
                    po, tp = bank()
                    for kc in range(8):
                        mm(psum[:, po:po + 24], hnT[:, kc, c * CH:(c + 1) * CH], wsm[:, kc, :], R=[t_hnT, t_wsm], W=[tp], start=(kc == 0), stop=(kc == 7))
                    sm, tsm = sm_p.get()
                    t1, tt1 = p64f.get()
                    op("dve", lambda e: e.tensor_tensor(out=t1[:, 0:24], in0=psum[:, po:po + 24], in1=rows[:, ROW_SMB:ROW_SMB + 24], op=ALU.add),
                       R=[tp, t_rows], W=[tt1])
                    op("act", lambda e: e.activation(out=t1[:, 24:48], in_=t1[:, 0:24], func=AF.Exp), R=[tt1], W=[tt1])
                    op("act", lambda e: e.activation(out=t1[:, 24:48], in_=t1[:, 24:48], func=AF.Ln, bias=1.0), R=[tt1], W=[tt1])
                    op("act", lambda e: e.activation(out=sm[:, 72:78], in_=t1[:, 6:12], func=AF.Sigmoid), R=[tt1], W=[tsm])
                    op("dve", lambda e: e.tensor_scalar(sm[:, 78:84], sm[:, 72:78], -1.0, None, op0=ALU.mult), R=[tsm], W=[tsm])
                    op("pool", lambda e: e.tensor_copy(out=sm[:, 84:96], in_=t1[:, 36:48]), R=[tt1], W=[tsm])
                    if ti == 0:
                        op("pool", lambda e: e.memset(sm[0:96, 84:96], 0.0), W=[tsm])
                        op("pool", lambda e: e.memset(sm[96:112, 84:96], 0.0), W=[tsm])
                    op("dve", lambda e: e.tensor_tensor(out=sm[:, 0:6], in0=t1[:, 24:30], in1=nega[:, 0:6], op=ALU.mult), R=[tt1, t_rows], W=[tsm])
                    op("dve", lambda e: e.tensor_tensor(out=sm[:, 6:18], in0=t1[:, 36:48], in1=nega[:, 6:18], op=ALU.mult), R=[tt1, t_rows], W=[tsm])
                    po2, tp2 = bank()
                    mm(psum[:, po2:po2 + 18], CST("tri"), sm[:, 0:18], R=[tsm] + TC, W=[tp2])
                    mm(psum[:, po2 + 32:po2 + 50], onesf[:, :], sm[:, 0:18], R=[tsm] + TC, W=[tp2])
                    op("dve", lambda e: e.tensor_copy(out=sm[:, 18:36], in_=psum[:, po2:po2 + 18]), R=[tp2], W=[tsm])
                    op("act", lambda e: e.activation(out=sm[:, 36:54], in_=psum[:, po2:po2 + 18], func=AF.Exp), R=[tp2], W=[tsm])
                    eg_, teg = egt_p.get()
                    op("act", lambda e: e.activation(out=eg_[:, 0:18], in_=psum[:, po2 + 32:po2 + 50], func=AF.Exp), R=[tp2], W=[teg])
                    op("dve", lambda e: e.tensor_tensor(out=sm[:, 54:72], in0=psum[:, po2 + 32:po2 + 50], in1=sm[:, 18:36], op=ALU.subtract),
                       R=[tp2, tsm], W=[tsm])
                    op("act", lambda e: e.activation(out=sm[:, 54:72], in_=sm[:, 54:72], func=AF.Exp), R=[tsm], W=[tsm])
                    smc.append((sm, tsm))
                    egt.append((eg_, teg))

                for h in range(H_A if flags["gdn"] else 0):
                    qkv = []
                    for pi, pn in enumerate("qkv"):
                        o_, to_ = big_p.get()
                        proj_conv(l, pi * D_A + h * 128, T, carry_g[h][pi], ["gcw_%s%d_%d" % (pn, h, j) for j in range(4)], None, o_[:, 0:T], to_)
                        qkv.append((o_, to_))
                    zs, tzs = big_p.get()
                    proj_block(l, 3 * D_A + h * 128, T, lambda ps, tp: op("act", lambda e: e.activation(out=zs[:, 0:T], in_=ps, func=AF.Silu), R=[tp], W=[tzs]))
                    hat = []
                    for (src, tsrc), scale in ((qkv[0], DH_A ** -0.5), (qkv[1], 1.0)):
                        sq, tsq = bigb_p.get()
                        op("pool", lambda e: e.tensor_tensor(out=sq[:, 0:T], in0=src[:, 0:T], in1=src[:, 0:T], op=ALU.mult), R=[tsrc], W=[tsq])
                        po, tp = bank()
                        mm(psum[:, po:po + T], onesb[:, :], sq[:, 0:T], R=[tsq] + TC, W=[tp])
                        ri, tri_ = big_p.get()
                        op("act", lambda e: e.activation(out=ri[:, 0:T], in_=psum[:, po:po + T], func=AF.Sqrt, bias=1e-6), R=[tp], W=[tri_])
                        op("dve", lambda e: e.reciprocal(out=ri[:, 0:T], in_=ri[:, 0:T]), R=[tri_], W=[tri_])
                        hb_, thb = bigb_p.get()
                        op("dve", lambda e: e.scalar_tensor_tensor(out=hb_[:, 0:T], in0=src[:, 0:T], scalar=float(scale), in1=ri[:, 0:T], op0=ALU.mult, op1=ALU.mult),
                           R=[tsrc, tri_], W=[thb])
                        hat.append((hb_, thb))
                    (qh, tqh), (kh, tkh) = hat
                    vb, tvb = bigb_p.get()
                    op("pool", lambda e: e.tensor_copy(out=vb[:, 0:T], in_=qkv[2][0][:, 0:T]), R=[qkv[2][1]], W=[tvb])
                    s32, s16, tS = Sg[h]
                    def gdn_phase1(c):
                        cs = slice(c * CH, (c + 1) * CH)
                        sm, tsm = smc[c]
                        eg_, teg = egt[c]
                        po, tp = bank()
                        mm(psum[:, po:po + 128], kh[:, cs], identb[:, :], R=[tkh] + TC, W=[tp])
                        mm(psum[:, po + 128:po + 256], vb[:, cs], identb[:, :], R=[tvb] + TC, W=[tp])
                        Y0, ty0 = p256f.get()
                        op("act", lambda e: e.activation(out=Y0[:, 0:128], in_=psum[:, po + 128:po + 256], func=AF.Copy), R=[tp], W=[ty0])
                        op("dve", lambda e: e.tensor_scalar(Y0[:, 128:256], psum[:, po:po + 128], sm[:, 36 + h:37 + h], None, op0=ALU.mult),
                           R=[tp, tsm], W=[ty0])
                        kdec, tkdec = p128b.get()
                        op("act", lambda e: e.activation(out=kdec[:, :], in_=psum[:, po:po + 128], func=AF.Copy, scale=sm[:, 54 + h:55 + h]),
                           R=[tp, tsm], W=[tkdec])
                        po2, tp2 = bank()
                        mm(psum[:, po2:po2 + 128], kh[:, cs], kh[:, cs], R=[tkh], W=[tp2])
                        mm(psum[:, po2 + 128:po2 + 256], kh[:, cs], qh[:, cs], R=[tkh, tqh], W=[tp2])
                        DT, tDT = decay_mask(sm[:, h:h + 1], tsm)
                        DTs, tDTs = p128f.get()
                        op("pool", lambda e: e.tensor_tensor(out=DTs[:, :], in0=DT[:, :], in1=CST("strict"), op=ALU.mult), R=[tDT] + TC, W=[tDTs])
                        NT0, tNT0 = p128f.get()
                        op("dve", lambda e: e.scalar_tensor_tensor(out=NT0[:, :], in0=psum[:, po2:po2 + 128], scalar=sm[:, 78 + h:79 + h], in1=DTs[:, :],
                                                                   op0=ALU.mult, op1=ALU.mult), R=[tp2, tsm, tDTs], W=[tNT0])
                        attnT, tat = p128b.get()
                        op("dve", lambda e: e.tensor_tensor(out=attnT[:, :], in0=psum[:, po2 + 128:po2 + 256], in1=DT[:, :], op=ALU.mult), R=[tp2, tDT], W=[tat])
                        return dict(cs=cs, sm=sm, tsm=tsm, eg_=eg_, teg=teg, Y0=Y0, ty0=ty0, kdec=kdec, tkdec=tkdec, NT0=NT0, tNT0=tNT0, attnT=attnT, tat=tat)

                    def gdn_phase2(d, Y6, ty6):
                        cs, sm, tsm, eg_, teg, kdec, tkdec, attnT, tat = d["cs"], d["sm"], d["tsm"], d["eg_"], d["teg"], d["kdec"], d["tkdec"], d["attnT"], d["tat"]
                        Yb, tyb = p256f.get()
                        op("dve", lambda e: e.tensor_scalar(Yb[:, :], Y6[:, :], sm[:, 72 + h:73 + h], None, op0=ALU.mult), R=[ty6, tsm], W=[tyb])
                        po3, tp3 = bank()
                        mm(psum[:, po3:po3 + 128], Yb[:, 128:256], identf, R=[tyb] + TC, W=[tp3])
                        WT, tWT = p128b.get()
                        op("act", lambda e: e.activation(out=WT[:, :], in_=psum[:, po3:po3 + 128], func=AF.Copy), R=[tp3], W=[tWT])
                        po4, tp4 = bank()
                        mm(psum[:, po4:po4 + 128], WT[:, :], s16[:, :], R=[tWT, tS], W=[tp4])
                        mm(psum[:, po4 + 128:po4 + 256], qh[:, cs], s16[:, :], R=[tqh, tS], W=[tp4])
                        vn, tvn = p128b.get()
                        op("dve", lambda e: e.tensor_tensor(out=vn[:, :], in0=Yb[:, 0:128], in1=psum[:, po4:po4 + 128], op=ALU.subtract), R=[tyb, tp4], W=[tvn])
                        oa, toa = p128f.get()
                        op("act", lambda e: e.activation(out=oa[:, :], in_=psum[:, po4 + 128:po4 + 256], func=AF.Copy, scale=sm[:, 36 + h:37 + h]),
                           R=[tp4, tsm], W=[toa])
                        po5, tp5 = bank()
                        mm(psum[:, po5:po5 + 128], attnT[:, :], vn[:, :], R=[tat, tvn], W=[tp5])
                        o_, to_ = p128f.get()
                        op("dve", lambda e: e.tensor_tensor(out=o_[:, :], in0=oa[:, :], in1=psum[:, po5:po5 + 128], op=ALU.add), R=[toa, tp5], W=[to_])
                        po6, tp6 = bank()
                        mm(psum[:, po6:po6 + 128], kdec[:, :], vn[:, :], R=[tkdec, tvn], W=[tp6])
                        op("dve", lambda e: e.scalar_tensor_tensor(out=s32[:, :], in0=s32[:, :], scalar=eg_[:, h:h + 1], in1=psum[:, po6:po6 + 128],
                                                                   op0=ALU.mult, op1=ALU.add), R=[tS, teg, tp6], W=[tS])
                        op("pool", lambda e: e.tensor_copy(out=s16[:, :], in_=s32[:, :]), R=[tS], W=[tS])
                        cl, tcl = col_p.get()
                        jk, tjk = p128f.get()
                        op("act", lambda e: e.activation(out=jk[:, :], in_=o_[:, :], func=AF.Square, accum_out=cl[:, 0:1]), R=[to_], W=[tjk, tcl])
                        rstd_from_ss(cl[:, 0:1], cl[:, 1:2], DH_A, 1e-6, [tcl], tcl)
                        on, ton = p128f.get()
                        op("dve", lambda e: e.tensor_scalar(on[:, :], o_[:, :], cl[:, 1:2], None, op0=ALU.mult), R=[to_, tcl], W=[ton])
                        po7, tp7 = bank()
                        mm(psum[:, po7:po7 + 128], on[:, :], identf, R=[ton] + TC, W=[tp7])
                        op("dve", lambda e: e.scalar_tensor_tensor(out=mixedT[:, h, cs], in0=psum[:, po7:po7 + 128], scalar=PV("gnorm"), in1=zs[:, cs],
                                                                   op0=ALU.mult, op1=ALU.mult), R=[tp7, t_pv, tzs], W=[t_mixed[h]])
                    for c0 in range(0, NCH, 2):
                        ds = [gdn_phase1(c) for c in range(c0, min(c0 + 2, NCH))]
                        ys = neumann_multi([(d["NT0"][:, :], d["tNT0"], d["Y0"], d["ty0"], 256) for d in ds])
                        for d, (Y6, ty6) in zip(ds, ys):
                            gdn_phase2(d, Y6, ty6)
                if not flags["gdn"]:
                    for h in range(H_A):
                        op("pool", lambda e, h=h: e.memset(mixedT[:, h, 0:T], 0.0), W=[t_mixed[h]])

                if flags["rwkv"]:
                    lo, tlo = lo_t
                    xl, txl = big_p.get()
                    proj_shift(l, OFF_B + 4 * D_B, T, carry_l, "mu_lora", xl[:, 0:T], txl)
                    op("act", lambda e: e.activation(out=lo[0:64, 0:T], in_=xl[0:64, 0:T], func=AF.Tanh), R=[txl], W=[tlo])
                    op("act", lambda e: e.activation(out=lo[64:128, 0:T], in_=xl[64:128, 0:T], func=AF.Copy), R=[txl], W=[tlo])
                for j in range(4 if flags["rwkv"] else 0):
                    mb = H_A + j
                    parts = {}
                    for pi, pn in enumerate("rkvg"):
                        o_, to_ = big_p.get()
                        proj_shift(l, OFF_B + pi * D_B + j * 128, T, carry_r[j][pi], "mu_%s%d" % (pn, j), o_[:, 0:T], to_)
                        parts[pn] = (o_, to_)
                    r_, tr_ = parts["r"]
                    k_, tk_ = parts["k"]
                    v_, tv_ = parts["v"]
                    g_, tg_ = parts["g"]
                    gs, tgs = big_p.get()
                    op("act", lambda e: e.activation(out=gs[:, 0:T], in_=g_[:, 0:T], func=AF.Silu), R=[tg_], W=[tgs])
                    po, tp = bank()
                    mm(psum[:, po:po + T], lwb[0:64, j * 128:(j + 1) * 128], lo[0:64, 0:T], R=[t_lw, tlo], W=[tp])
                    logw, tlogw = big_p.get()
                    op("act", lambda e: e.activation(out=logw[:, 0:T], in_=psum[:, po:po + T], func=AF.Sigmoid, bias=PV("w0_%d" % j)), R=[tp, t_pv], W=[tlogw])
                    op("dve", lambda e: e.tensor_scalar(logw[:, 0:T], logw[:, 0:T], -0.6065306597126334, None, op0=ALU.mult), R=[tlogw], W=[tlogw])
                    po, tp = bank(hi=True)
                    mm(psum[:, po:po + T], lwb[64:128, j * 128:(j + 1) * 128], lo[64:128, 0:T], R=[t_lw, tlo], W=[tp])
                    asig, tasig = big_p.get()
                    op("act", lambda e: e.activation(out=asig[:, 0:T], in_=psum[:, po:po + T], func=AF.Sigmoid, bias=PV("a0_%d" % j)), R=[tp, t_pv], W=[tasig])
                    kkk, tkkk = big_p.get()
                    op("dve", lambda e: e.tensor_scalar(kkk[:, 0:T], k_[:, 0:T], PV("kk_%d" % j), None, op0=ALU.mult), R=[tk_, t_pv], W=[tkkk])
                    sq, tsq = bigb_p.get()
                    op("pool", lambda e: e.tensor_tensor(out=sq[:, 0:T], in0=kkk[:, 0:T], in1=kkk[:, 0:T], op=ALU.mult), R=[tkkk], W=[tsq])
                    po, tp = bank()
                    mm(psum[:, po:po + T], blk2b[:, :], sq[:, 0:T], R=[tsq] + TC, W=[tp])
                    ri, tri_ = big_p.get()
                    op("act", lambda e: e.activation(out=ri[:, 0:T], in_=psum[:, po:po + T], func=AF.Sqrt, bias=1e-6), R=[tp], W=[tri_])
                    op("dve", lambda e: e.reciprocal(out=ri[:, 0:T], in_=ri[:, 0:T]), R=[tri_], W=[tri_])
                    kk, tkk = big_p.get()
                    op("pool", lambda e: e.tensor_tensor(out=kk[:, 0:T], in0=kkk[:, 0:T], in1=ri[:, 0:T], op=ALU.mult), R=[tkkk, tri_], W=[tkk])
                    kmod, tkmod = big_p.get()
                    op("dve", lambda e: e.tensor_scalar(kmod[:, 0:T], asig[:, 0:T], PV("ka_%d" % j), PVX(j), op0=ALU.mult, op1=ALU.add), R=[tasig, t_pv], W=[tkmod])
                    op("pool", lambda e: e.tensor_tensor(out=kmod[:, 0:T], in0=kmod[:, 0:T], in1=k_[:, 0:T], op=ALU.mult), R=[tkmod, tk_], W=[tkmod])
                    bvec, tbvec = big_p.get()
                    op("pool", lambda e: e.tensor_tensor(out=bvec[:, 0:T], in0=kk[:, 0:T], in1=asig[:, 0:T], op=ALU.mult), R=[tkk, tasig], W=[tbvec])
                    gc, tgc = big_p.get()
                    op("dve", lambda e: e.tensor_tensor_scan(out=gc[:, 0:T], data0=CST("cmask")[:, 0:T], data1=logw[:, 0:T], initial=0.0, op0=ALU.mult, op1=ALU.add),
                       R=[tlogw] + TC, W=[tgc])
                    eg, teg2 = big_p.get()
                    op("act", lambda e: e.activation(out=eg[:, 0:T], in_=gc[:, 0:T], func=AF.Exp), R=[tgc], W=[teg2])
                    egm, tegm = big_p.get()
                    op("act", lambda e: e.activation(out=egm[:, 0:T], in_=gc[:, 0:T], func=AF.Exp, scale=-1.0), R=[tgc], W=[tegm])
                    egx, tegx = big_p.get()
                    op("pool", lambda e: e.tensor_tensor(out=egx[:, 0:T], in0=gc[:, 0:T], in1=logw[:, 0:T], op=ALU.subtract), R=[tgc, tlogw], W=[tegx])
                    op("act", lambda e: e.activation(out=egx[:, 0:T], in_=egx[:, 0:T], func=AF.Exp), R=[tegx], W=[tegx])
                    rt, trt = bigb_p.get()
                    op("pool", lambda e: e.tensor_tensor(out=rt[:, 0:T], in0=r_[:, 0:T], in1=eg[:, 0:T], op=ALU.mult), R=[tr_, teg2], W=[trt])
                    kt, tkt = bigb_p.get()
                    op("dve", lambda e: e.tensor_tensor(out=kt[:, 0:T], in0=kmod[:, 0:T], in1=egm[:, 0:T], op=ALU.mult), R=[tkmod, tegm], W=[tkt])
                    bt, tbt = bigb_p.get()
                    op("pool", lambda e: e.tensor_tensor(out=bt[:, 0:T], in0=bvec[:, 0:T], in1=egm[:, 0:T], op=ALU.mult), R=[tbvec, tegm], W=[tbt])
                    at, tat_ = bigb_p.get()
                    op("dve", lambda e: e.scalar_tensor_tensor(out=at[:, 0:T], in0=kk[:, 0:T], scalar=-1.0, in1=egx[:, 0:T], op0=ALU.mult, op1=ALU.mult),
                       R=[tkk, tegx], W=[tat_])
                    vb, tvb = bigb_p.get()
                    op("pool", lambda e: e.tensor_copy(out=vb[:, 0:T], in_=v_[:, 0:T]), R=[tv_], W=[tvb])
                    prod, tprod = bigb_p.get()
                    op("dve", lambda e: e.scalar_tensor_tensor(out=prod[:, 0:T], in0=r_[:, 0:T], scalar=PV("rk_%d" % j), in1=kmod[:, 0:T], op0=ALU.mult, op1=ALU.mult),
                       R=[tr_, tkmod, t_pv], W=[tprod])
                    s32, s16, tS = Sr[j]
                    for c in range(NCH):
                        cs = slice(c * CH, (c + 1) * CH)
                        lastc = slice(c * CH + CH - 1, c * CH + CH)
                        edc, tedc = p128f.get()
                        op("act", lambda e: e.activation(out=edc[:, :], in_=gc[:, cs], func=AF.Exp, scale=-1.0, bias=gc[:, lastc]), R=[tgc], W=[tedc])
                        khat, tkhat = p128b.get()
                        op("pool", lambda e: e.tensor_tensor(out=khat[:, :], in0=kmod[:, cs], in1=edc[:, :], op=ALU.mult), R=[tkmod, tedc], W=[tkhat])
                        bhat, tbhat = p128b.get()
                        op("pool", lambda e: e.tensor_tensor(out=bhat[:, :], in0=bvec[:, cs], in1=edc[:, :], op=ALU.mult), R=[tbvec, tedc], W=[tbhat])
                        gcl, tgcl = col_p.get()
                        op("act", lambda e: e.activation(out=gcl[:, 0:1], in_=gc[:, lastc], func=AF.Exp), R=[tgc], W=[tgcl])
                        def rw_phase1(e_):
                            pb = 64 * e_
                            hi = bool(e_)
                            q = slice(pb, pb + 64)
                            hcol = (2 * j + e_) * 64
                            identq = identb[q, pb:pb + 64]
                            po, tp = bank(hi)
                            mm(psum[:, po:po + 128], bt[q, cs], at[q, cs], R=[tbt, tat_], W=[tp])
                            mm(psum[:, po + 128:po + 256], kt[q, cs], at[q, cs], R=[tkt, tat_], W=[tp])
                            mm(psum[:, po + 256:po + 384], bt[q, cs], rt[q, cs], R=[tbt, trt], W=[tp])
                            mm(psum[:, po + 384:po + 512], kt[q, cs], rt[q, cs], R=[tkt, trt], W=[tp])
                            NabT, tnab = p128f.get()
                            op("dve", lambda e: e.tensor_tensor(out=NabT[:, :], in0=psum[:, po:po + 128], in1=CST("strict"), op=ALU.mult), R=[tp] + TC, W=[tnab])
                            NB, tnb = p384b.get()
                            op("dve", lambda e: e.tensor_tensor(out=NB[:, :], in0=psum[:, po + 128:po + 512], in1=CST("mask4")[:, 128:512], op=ALU.mult), R=[tp] + TC, W=[tnb])
                            po2, tp2 = bank(hi)
                            mm(psum[:, po2:po2 + 64], vb[q, cs], identq, R=[tvb] + TC, W=[tp2])
                            mm(psum[:, po2 + 64:po2 + 128], at[q, cs], identq, R=[tat_] + TC, W=[tp2])
                            mm(psum[:, po2 + 128:po2 + 192], khat[q, :], identq, R=[tkhat] + TC, W=[tp2])
                            mm(psum[:, po2 + 192:po2 + 256], bhat[q, :], identq, R=[tbhat] + TC, W=[tp2])
                            mm(psum[:, po2 + 256:po2 + 257], prod[q, cs], onesb[q, 0:1], R=[tprod] + TC, W=[tp2])
                            mm(psum[:, po2 + 320:po2 + 384], rt[q, cs], s16[q, :], R=[trt, tS], W=[tp2])
                            Vt, tvt = p64b.get()
                            op("act", lambda e: e.activation(out=Vt[:, :], in_=psum[:, po2:po2 + 64], func=AF.Copy), R=[tp2], W=[tvt])
                            Vf, tvf = p64f.get()
                            op("dve", lambda e: e.tensor_copy(out=Vf[:, :], in_=psum[:, po2:po2 + 64]), R=[tp2], W=[tvf])
                            KBt, tkbt = p128b.get()
                            op("act", lambda e: e.activation(out=KBt[:, :], in_=psum[:, po2 + 128:po2 + 256], func=AF.Copy), R=[tp2], W=[tkbt])
                            Y0, ty0 = p256f.get()
                            op("dve", lambda e: e.tensor_copy(out=Y0[:, 64:128], in_=psum[:, po2 + 64:po2 + 128]), R=[tp2], W=[ty0])
                            cl, tcl = col_p.get()
                            op("dve", lambda e: e.tensor_copy(out=cl[:, 6:7], in_=psum[:, po2 + 256:po2 + 257]), R=[tp2], W=[tcl])
                            rs0, trs0 = p64f.get()
                            op("act", lambda e: e.activation(out=rs0[:, :], in_=psum[:, po2 + 320:po2 + 384], func=AF.Copy), R=[tp2], W=[trs0])
                            po3, tp3 = bank()
                            mm(psum[:, po3:po3 + 64], NB[:, 0:128], Vt[:, :], R=[tnb, tvt], W=[tp3])
                            op("act", lambda e: e.activation(out=Y0[:, 0:64], in_=psum[:, po3:po3 + 64], func=AF.Copy), R=[tp3], W=[ty0])
                            return dict(pb=pb, hi=hi, q=q, hcol=hcol, NabT=NabT, tnab=tnab, NB=NB, tnb=tnb, Vt=Vt, tvt=tvt, Vf=Vf, tvf=tvf, KBt=KBt, tkbt=tkbt,
                                        Y0=Y0, ty0=ty0, cl=cl, tcl=tcl, rs0=rs0, trs0=trs0)

                        def rw_phase2(d, Y6, ty6):
                            pb, hi, q, hcol, NB, tnb, Vt, tvt, Vf, tvf, KBt, tkbt, cl, tcl, rs0, trs0 = (d["pb"], d["hi"], d["q"], d["hcol"], d["NB"], d["tnb"], d["Vt"], d["tvt"],
                                                                                                           d["Vf"], d["tvf"], d["KBt"], d["tkbt"], d["cl"], d["tcl"], d["rs0"], d["trs0"])
                            po4, tp4 = bank()
                            mm(psum[q, po4:po4 + 128], Y6[:, 64:128], identf, R=[ty6] + TC, W=[tp4])
                            YaT, tyat = p128b.get()
                            op("act", lambda e: e.activation(out=YaT[q, :], in_=psum[q, po4:po4 + 128], func=AF.Copy), R=[tp4], W=[tyat])
                            po5, tp5 = bank(hi)
                            mm(psum[:, po5:po5 + 64], YaT[q, :], s16[q, :], R=[tyat, tS], W=[tp5])
                            Ub, tub = p64b.get()
                            op("dve", lambda e: e.tensor_tensor(out=Ub[:, :], in0=psum[:, po5:po5 + 64], in1=Y6[:, 0:64], op=ALU.add), R=[tp5, ty6], W=[tub])
                            po6, tp6 = bank()
                            mm(psum[:, po6:po6 + 64], NB[:, 128:256], Ub[:, :], R=[tnb, tub], W=[tp6], start=True, stop=False)
                            mm(psum[:, po6:po6 + 64], NB[:, 256:384], Vt[:, :], R=[tnb, tvt], W=[tp6], start=False, stop=True)
                            mm(psum[q, po6 + 64:po6 + 128], KBt[:, 64:128], Ub[:, :], R=[tkbt, tub], W=[tp6], start=True, stop=False)
                            mm(psum[q, po6 + 64:po6 + 128], KBt[:, 0:64], Vt[:, :], R=[tkbt, tvt], W=[tp6], start=False, stop=True)
                            y_, ty_ = p64f.get()
                            op("dve", lambda e: e.tensor_tensor(out=y_[:, :], in0=psum[:, po6:po6 + 64], in1=rs0[:, :], op=ALU.add), R=[tp6, trs0], W=[ty_])
                            op("dve", lambda e: e.scalar_tensor_tensor(out=s32[q, :], in0=s32[q, :], scalar=gcl[q, 0:1], in1=psum[q, po6 + 64:po6 + 128],
                                                                       op0=ALU.mult, op1=ALU.add), R=[tS, tgcl, tp6], W=[tS])
                            op("pool", lambda e: e.tensor_copy(out=s16[q, :], in_=s32[q, :]), R=[tS], W=[tS])
                            jk, tjk = p64f.get()
                            op("act", lambda e: e.activation(out=jk[:, :], in_=y_[:, :], func=AF.Identity, accum_out=cl[:, 1:2]), R=[ty_], W=[tjk, tcl])
                            op("act", lambda e: e.activation(out=jk[:, :], in_=y_[:, :], func=AF.Square, accum_out=cl[:, 2:3]), R=[ty_, tjk], W=[tjk, tcl])
                            op("dve", lambda e: e.tensor_scalar(cl[:, 3:4], cl[:, 1:2], -1.0 / 64, None, op0=ALU.mult), R=[tcl], W=[tcl])
                            op("dve", lambda e: e.tensor_tensor(out=cl[:, 4:5], in0=cl[:, 3:4], in1=cl[:, 3:4], op=ALU.mult), R=[tcl], W=[tcl])
                            op("dve", lambda e: e.scalar_tensor_tensor(out=cl[:, 4:5], in0=cl[:, 2:3], scalar=1.0 / 64, in1=cl[:, 4:5], op0=ALU.mult, op1=ALU.subtract),
                               R=[tcl], W=[tcl])
                            rstd_from_ss(cl[:, 4:5], cl[:, 5:6], 1.0, 64e-5, [tcl], tcl)
                            yn, tyn = p64f.get()
                            op("dve", lambda e: e.tensor_scalar(yn[:, :], y_[:, :], cl[:, 3:4], cl[:, 5:6], op0=ALU.add, op1=ALU.mult), R=[ty_, tcl], W=[tyn])
                            op("pool", lambda e: e.tensor_tensor(out=yn[:, :], in0=yn[:, :], in1=rows[:, ROW_LNW + hcol:ROW_LNW + hcol + 64], op=ALU.mult), R=[tyn, t_rows], W=[tyn])
                            op("pool", lambda e: e.tensor_tensor(out=yn[:, :], in0=yn[:, :], in1=rows[:, ROW_LNB + hcol:ROW_LNB + hcol + 64], op=ALU.add), R=[tyn, t_rows], W=[tyn])
                            y4, ty4 = p64f.get()
                            op("dve", lambda e: e.scalar_tensor_tensor(out=y4[:, :], in0=Vf[:, :], scalar=cl[:, 6:7], in1=yn[:, :], op0=ALU.mult, op1=ALU.add),
                               R=[tvf, tcl, tyn], W=[ty4])
                            po8, tp8 = bank()
                            mm(psum[q, po8:po8 + 128], y4[:, :], identf, R=[ty4] + TC, W=[tp8])
                            op("dve", lambda e: e.tensor_tensor(out=mixedT[q, mb, cs], in0=psum[q, po8:po8 + 128], in1=gs[q, cs], op=ALU.mult), R=[tp8, tgs], W=[t_mixed[mb]])
                        rds = [rw_phase1(0), rw_phase1(1)]
                        rys = neumann_multi([(d["NabT"][:, :], d["tnab"], d["Y0"], d["ty0"], 128) for d in rds])
                        for d, (Y6, ty6) in zip(rds, rys):
                            rw_phase2(d, Y6, ty6)
                if not flags["rwkv"]:
                    for j in range(4):
                        op("pool", lambda e, j=j: e.memset(mixedT[:, H_A + j, 0:T], 0.0), W=[t_mixed[H_A + j]])

                if flags["ssd"]:
                    xf = []
                    zf = []
                    for b in range(6):
                        o_, to_ = big_p.get()
                        proj_conv(l, OFF_C + D_C + b * 128, T, carry_m[b], ["mcw%d_%d" % (b, j) for j in range(4)], "mcb%d" % b, o_[:, 0:T], to_)
                        xf.append((o_, to_))
                        z_, tz_ = big_p.get()
                        proj_block(l, OFF_C + b * 128, T, lambda ps, tp, z_=z_, tz_=tz_: op("act", lambda e: e.activation(out=z_[:, 0:T], in_=ps, func=AF.Silu), R=[tp], W=[tz_]))
                        zf.append((z_, tz_))
                    Bf = []
                    Cf = []
                    for g in range(4):
                        for which, lst in ((0, Bf), (1, Cf)):
                            cb = 6 + which * 4 + g
                            o_, to_ = bigb_p.get()
                            proj_conv(l, OFF_C + D_C + cb * 128, T, carry_m[cb], ["mcw%d_%d" % (cb, j) for j in range(4)], "mcb%d" % cb, o_[:, 0:T], to_)
                            lst.append((o_, to_))
                    for c in range(NCH):
                        cs = slice(c * CH, (c + 1) * CH)
                        sm, tsm = smc[c]
                        eg_, teg = egt[c]
                        ytok, tyt = ytok_p.get()
                        xtoks = []
                        for b in range(6):
                            po, tp = bank()
                            mm(psum[:, po:po + 128], xf[b][0][:, cs], identf, R=[xf[b][1]] + TC, W=[tp])
                            xt, txt = xt_p.get()
                            op("act", lambda e: e.activation(out=xt[:, :], in_=psum[:, po:po + 128], func=AF.Copy), R=[tp], W=[txt])
                            xd, txd = xd_p.get()
                            for e_ in range(2):
                                hh = 2 * b + e_
                                op("dve", lambda e, e_=e_, hh=hh: e.tensor_scalar(xd[:, e_ * 64:(e_ + 1) * 64], psum[:, po + e_ * 64:po + (e_ + 1) * 64],
                                                                                   sm[:, 84 + hh:85 + hh], None, op0=ALU.mult), R=[tp, tsm], W=[txd])
                            xtoks.append((xt, txt, xd, txd))
                        for g in range(4):
                            Bt_, tB = Bf[g]
                            Ct_, tC = Cf[g]
                            po, tp = bank()
                            mm(psum[:, po:po + 128], Bt_[:, cs], Ct_[:, cs], R=[tB, tC], W=[tp])
                            mm(psum[:, po + 128:po + 256], Bt_[:, cs], identb[:, :], R=[tB] + TC, W=[tp])
                            CBT, tcbt = p128f.get()
                            op("act", lambda e: e.activation(out=CBT[:, :], in_=psum[:, po:po + 128], func=AF.Copy), R=[tp], W=[tcbt])
                            bdecs = []
                            for hh in range(3 * g, 3 * g + 3):
                                bdec, tbdec = p128b.get()
                                op("act", lambda e, hh=hh, bdec=bdec: e.activation(out=bdec[:, :], in_=psum[:, po + 128:po + 256], func=AF.Copy,
                                                                                   scale=sm[:, 54 + 6 + hh:55 + 6 + hh]), R=[tp, tsm], W=[tbdec])
                                bdecs.append((bdec, tbdec))
                            for hh in range(3 * g, 3 * g + 3):
                                b, e_ = hh // 2, hh % 2
                                xt, txt, xd, txd = xtoks[b]
                                s32, s16, tS = Sm[hh]
                                bdec, tbdec = bdecs[hh - 3 * g]
                                LT, tLT = decay_mask(sm[:, 6 + hh:7 + hh], tsm)
                                MT, tMT = p128b.get()
                                op("pool", lambda e: e.tensor_tensor(out=MT[:, :], in0=CBT[:, :], in1=LT[:, :], op=ALU.mult), R=[tcbt, tLT], W=[tMT])
                                po2, tp2 = bank()
                                mm(psum[:, po2:po2 + 64], MT[:, :], xd[:, e_ * 64:(e_ + 1) * 64], R=[tMT, txd], W=[tp2])
                                mm(psum[:, po2 + 64:po2 + 128], Ct_[:, cs], s16[:, :], R=[tC, tS], W=[tp2])
                                mm(psum[:, po2 + 128:po2 + 192], bdec[:, :], xd[:, e_ * 64:(e_ + 1) * 64], R=[tbdec, txd], W=[tp2])
                                ya, tya = p64f.get()
                                op("act", lambda e: e.activation(out=ya[:, :], in_=psum[:, po2 + 64:po2 + 128], func=AF.Copy, scale=sm[:, 36 + 6 + hh:37 + 6 + hh]),
                                   R=[tp2, tsm], W=[tya])
                                op("dve", lambda e: e.tensor_tensor(out=ya[:, :], in0=ya[:, :], in1=psum[:, po2:po2 + 64], op=ALU.add), R=[tya, tp2], W=[tya])
                                op("dve", lambda e: e.scalar_tensor_tensor(out=ytok[:, hh * 64:(hh + 1) * 64], in0=xt[:, e_ * 64:(e_ + 1) * 64],
                                                                           scalar=rows[:, ROW_D + hh:ROW_D + hh + 1], in1=ya[:, :], op0=ALU.mult, op1=ALU.add),
                                   R=[txt, t_rows, tya], W=[tyt])
                                op("dve", lambda e: e.scalar_tensor_tensor(out=s32[:, :], in0=s32[:, :], scalar=eg_[:, 6 + hh:7 + hh], in1=psum[:, po2 + 128:po2 + 192],
                                                                           op0=ALU.mult, op1=ALU.add), R=[tS, teg, tp2], W=[tS])
                                op("pool", lambda e: e.tensor_copy(out=s16[:, :], in_=s32[:, :]), R=[tS], W=[tS])
                        for b in range(6):
                            po, tp = bank()
                            mm(psum[:, po:po + 128], zf[b][0][:, cs], identf, R=[zf[b][1]] + TC, W=[tp])
                            op("dve", lambda e, b=b, po=po: e.tensor_tensor(out=ytok[:, b * 128:(b + 1) * 128], in0=ytok[:, b * 128:(b + 1) * 128],
                                                                            in1=psum[:, po:po + 128], op=ALU.mult), R=[tyt, tp], W=[tyt])
                        cl, tcl = col_p.get()
                        jk, tjk = p256f.get()
                        for g in range(4):
                            op("act", lambda e, g=g: e.activation(out=jk[:, 0:192], in_=ytok[:, g * 192:(g + 1) * 192], func=AF.Square, accum_out=cl[:, g:g + 1]),
                               R=[tyt], W=[tjk, tcl])
                        tmp, ttmp = col_p.get()
                        op("act", lambda e: e.activation(out=tmp[:, 0:4], in_=cl[:, 0:4], func=AF.Sqrt, scale=1.0 / 192, bias=1e-6), R=[tcl], W=[ttmp])
                        op("dve", lambda e: e.reciprocal(out=cl[:, 4:8], in_=tmp[:, 0:4]), R=[ttmp], W=[tcl])
                        for g in range(4):
                            op("dve", lambda e, g=g: e.tensor_scalar(ytok[:, g * 192:(g + 1) * 192], ytok[:, g * 192:(g + 1) * 192], cl[:, 4 + g:5 + g], None, op0=ALU.mult),
                               R=[tcl, tyt], W=[tyt])
                        for b in range(6):
                            po, tp = bank()
                            mm(psum[:, po:po + 128], ytok[:, b * 128:(b + 1) * 128], identf, R=[tyt] + TC, W=[tp])
                            op("act", lambda e, b=b, po=po: e.activation(out=mixedT[:, 10 + b, cs], in_=psum[:, po:po + 128], func=AF.Copy, scale=PV("mnorm%d" % b)),
                               R=[tp, t_pv], W=[t_mixed[10 + b]])
                else:
                    for b in range(6):
                        op("pool", lambda e, b=b: e.memset(mixedT[:, 10 + b, 0:T], 0.0), W=[t_mixed[10 + b]])

                if debug_mixed:
                    for b in range(16):
                        mf, tmf = big_p.get()
                        op("dve", lambda e, b=b, mf=mf: e.tensor_copy(out=mf[:, 0:T], in_=mixedT[:, b, 0:T]), R=[t_mixed[b]], W=[tmf])
                        dma(dbg_d[l, b][:, r0:r0 + T], mf[:, 0:T], R=[tmf], W=[Tok()])

                for si in range(nsub):
                    rs = slice(si * 128, (si + 1) * 128)
                    po, tps = bank2()
                    for half in range(2):
                        for b in range(16):
                            mm(psum[:, po + half * 512:po + (half + 1) * 512], mixedT[:, b, rs], woutb[:, b, half * 512:(half + 1) * 512],
                               R=[t_mixed[b], t_wout], W=[tps[half]], start=(b == 0), stop=(b == 15))
                    cl, tcl = col_p.get()
                    jk, tjk = xn_p.get()
                    op("act", lambda e: e.activation(out=jk[:, :], in_=psum[:, po:po + 1024], func=AF.Square, accum_out=cl[:, 0:1]), R=tps, W=[tjk, tcl])
                    rstd_from_ss(cl[:, 0:1], cl[:, 1:2], D, 1e-6, [tcl], tcl)
                    hs, ths = xin_p.get()
                    load_h(si, hs, ths)
                    op("dve", lambda e: e.scalar_tensor_tensor(out=jk[:, :], in0=psum[:, po:po + 1024], scalar=cl[:, 1:2], in1=rows[:, ROW_NPOST:ROW_NPOST + 1024],
                                                               op0=ALU.mult, op1=ALU.mult), R=tps + [tcl, t_rows, tjk], W=[tjk])
                    op("pool", lambda e: e.tensor_tensor(out=hs[:, :], in0=hs[:, :], in1=jk[:, :], op=ALU.add), R=[ths, tjk], W=[ths])
                    g0 = r0 + si * 128
                    if l == 0:
                        dma(h1_d[g0:g0 + 128, :], hs[:, :], R=[ths], W=[h1tok[ti][si]])
                    elif ti > 0:
                        dma(y_d[g0 - 128:g0, :], hs[:, :], R=[ths], W=[Tok()])

        for eng in ("pe", "dve", "act", "pool"):
            if S.cnt[eng]:
                sm_, v_ = S._sem(eng, S.cnt[eng] - 1)
                nc.sync.wait_ge(sm_, v_)
        for q in range(8):
            if S.dma_cnt[q]:
                nc.sync.wait_ge(S.dma_sems[q], S.dma_cnt[q])
    return nc


_CACHE = {}


def _prep(P, seq):
    cst, coffs = make_consts()
    regs = [pack_pv(l, P) for l in range(DEPTH)]
    pvn = regs[0].names
    npv = len(regs[0].cols)
    pv = np.stack([np.stack(r.cols, 1) for r in regs], 0).astype(np.float32)
    rows = np.stack([pack_rows(l, P) for l in range(DEPTH)], 0)
    lora = np.stack([np.concatenate([P["rw_w2"][l], P["rw_a2"][l]], 0) for l in range(DEPTH)], 0).astype(np.float32)
    return cst, coffs, pvn, npv, pv, rows, lora


def run(inputs, seq, n_cores, flags=None, debug_mixed=False):
    P = {k: np.asarray(v, np.float32) for k, v in inputs.items()}
    cst, coffs, pvn, npv, pv, rows, lora = _prep(P, seq)
    key = (seq, debug_mixed, str(flags))
    nc = build(seq, pvn, coffs, cst.shape[1], npv, flags=flags, debug_mixed=debug_mixed)
    nb = P["x"].shape[0]
    in_maps = []
    for i in range(n_cores):
        b = i % nb
        in_maps.append({
            "x": np.ascontiguousarray(P["x"][b]),
            "meta": P["meta_tokens"],
            "win": P["w_in"], "wout": P["w_out"],
            "pv": pv, "rows": rows, "lora": lora, "cst": cst,
        })
    res = run_bass_kernel_spmd(nc, in_maps, core_ids=list(range(n_cores)))
    return res


def kernel(**inputs):
    seq = inputs["x"].shape[1]
    res = run(inputs, seq, 8)
    out = np.stack([np.asarray(res.results[b]["y"]) for b in range(inputs["x"].shape[0])], 0)
    return out.astype(np.float32)
```
